# Optimizing a Trainium2 kernel written in Bass

```python
import math
import jax, jax.numpy as jnp
from jax import lax
import numpy as np

D_MODEL = 2048
BATCH = 4
SEQ = 4096
DEPTH = 2

MEM_LEN = 256
NORM_EPS = 1e-6

DA_HEADS = 8
DA_QK_DIM = 64
DA_V_DIM = 2 * DA_QK_DIM
DA_WIDTH = DA_HEADS * DA_V_DIM
Q_BLOCK = 128

HG_HEADS = 8
HG_DK = 128
HG_DV = 128
HG_WIDTH = HG_HEADS * HG_DV
HG_CHUNK = 64

MIX_WIDTH = DA_WIDTH + HG_WIDTH

DA_Q_COLS = DA_HEADS * 2 * DA_QK_DIM
DA_K_COLS = DA_HEADS * 2 * DA_QK_DIM
DA_V_COLS = DA_WIDTH
HG_KEY_COLS = HG_HEADS * HG_DK
HG_VAL_COLS = HG_HEADS * HG_DV
IN_SIZES = (DA_Q_COLS, DA_K_COLS, DA_V_COLS, HG_KEY_COLS, HG_KEY_COLS, HG_VAL_COLS, HG_VAL_COLS)
IN_COLS = sum(IN_SIZES)
IN_SPLITS = tuple(int(v) for v in np.cumsum(IN_SIZES)[:-1])

REL_BUCKETS = 32
REL_MAX_DIST = 128

CX_HEADS = 4
CX_HEAD_DIM = 128
CX_WIDTH = CX_HEADS * CX_HEAD_DIM

FFN_HIDDEN = ((8 * D_MODEL // 3 + 255) // 256) * 256

kernel_name = "hymba_style_diffattn_hgrn2_hybrid"


def rms_norm(x, g):
    xf = x.astype(jnp.float32)
    y = xf * lax.rsqrt(jnp.mean(xf * xf, axis=-1, keepdims=True) + NORM_EPS)
    return (y * g.astype(jnp.float32)).astype(x.dtype)


def t5_causal_bucket(dist):
    n = jnp.maximum(dist, 0)
    max_exact = REL_BUCKETS // 2
    nf = jnp.maximum(n, 1).astype(jnp.float32)
    large = max_exact + (jnp.log(nf / max_exact) / math.log(REL_MAX_DIST / max_exact)
                         * (REL_BUCKETS - max_exact)).astype(jnp.int32)
    large = jnp.minimum(large, REL_BUCKETS - 1)
    return jnp.where(n < max_exact, n, large)


def differential_attention(q, k, v, lam, lam_init, subln_g, rel_bias):
    B, S = q.shape[0], q.shape[1]
    n_blk = S // Q_BLOCK
    scale = DA_QK_DIM ** -0.5
    kt = jnp.transpose(k, (0, 2, 3, 1, 4))
    vt = jnp.transpose(v, (0, 2, 1, 3))
    qb = q.reshape(B, n_blk, Q_BLOCK, DA_HEADS, 2, DA_QK_DIM).transpose(1, 0, 3, 4, 2, 5)
    k_pos = jnp.arange(S)

    def block(args):
        q_blk, blk = args
        q_pos = blk * Q_BLOCK + jnp.arange(Q_BLOCK)
        dist = q_pos[:, None] - k_pos[None, :]
        bias = jnp.transpose(rel_bias[t5_causal_bucket(dist)], (2, 0, 1)).astype(jnp.float32)
        s = jnp.einsum('bhmqd,bhmkd->bhmqk', q_blk, kt).astype(jnp.float32) * scale + bias[None, :, None]
        s = jnp.where(dist >= 0, s, -jnp.inf)
        p = jax.nn.softmax(s, axis=-1)
        a = p[:, :, 0] - lam * p[:, :, 1]
        return jnp.einsum('bhqk,bhkd->bhqd', a.astype(vt.dtype), vt)

    o = lax.map(block, (qb, jnp.arange(n_blk)))
    o = rms_norm(o, subln_g) * (1.0 - lam_init)
    return o.transpose(1, 0, 3, 2, 4).reshape(B, S, DA_WIDTH)


def hgrn2(f_logit, q, i, g, lb, onorm_g):
    B, S = q.shape[0], q.shape[1]
    n_chunk = S // HG_CHUNK
    f = lb.astype(jnp.float32) + (1.0 - lb.astype(jnp.float32)) * jax.nn.sigmoid(f_logit.astype(jnp.float32))
    log_f = jnp.log(f)
    k = 1.0 - f

    def to_chunks(t, d):
        return t.astype(jnp.float32).reshape(B, n_chunk, HG_CHUNK, HG_HEADS, d).transpose(1, 0, 3, 2, 4)

    qc, kc, vc, gc = to_chunks(q, HG_DK), to_chunks(k, HG_DK), to_chunks(i, HG_DV), to_chunks(log_f, HG_DK)
    causal = jnp.tril(jnp.ones((HG_CHUNK, HG_CHUNK), dtype=bool))

    def step(state, inp):
        qt, kt, vt, gt = inp
        b = jnp.cumsum(gt, axis=2)
        diff = b[:, :, :, None, :] - b[:, :, None, :, :]
        decay = jnp.exp(jnp.where(causal[:, :, None], diff, -jnp.inf))
        attn = jnp.einsum('bhtc,bhsc,bhtsc->bhts', qt, kt, decay)
        o = (jnp.einsum('bhts,bhsv->bhtv', attn, vt)
             + jnp.einsum('bhtc,bhcv->bhtv', qt * jnp.exp(b), state))
        b_last = b[:, :, -1:, :]
        state = (jnp.exp(b_last[:, :, 0, :])[..., None] * state
                 + jnp.einsum('bhsc,bhsv->bhcv', kt * jnp.exp(b_last - b), vt))
        return state, o

    state0 = jnp.zeros((B, HG_HEADS, HG_DK, HG_DV), jnp.float32)
    _, outs = lax.scan(step, state0, (qc, kc, vc, gc))
    o = outs.transpose(1, 0, 3, 2, 4).reshape(B, S, HG_HEADS, HG_DV).astype(q.dtype)
    o = rms_norm(o, onorm_g) * jax.nn.silu(g.reshape(B, S, HG_HEADS, HG_DV))
    return o.reshape(B, S, HG_WIDTH)


def setup_inputs(seed: int = 0) -> dict:
    key = jax.random.key(seed)
    ks = jax.random.split(key, 24)

    def w(k, shape, fan_in):
        return jax.random.normal(k, shape, jnp.float32) * fan_in ** -0.5

    def gain(k, shape):
        return 1.0 + 0.02 * jax.random.normal(k, shape, jnp.float32)

    return {
        "x": jax.random.normal(ks[0], (BATCH, SEQ, D_MODEL), jnp.float32),
        "mem": jax.random.normal(ks[1], (BATCH, MEM_LEN, D_MODEL), jnp.float32),
        "w_in": w(ks[2], (DEPTH, D_MODEL, IN_COLS), D_MODEL),
        "w_out": w(ks[3], (DEPTH, MIX_WIDTH, D_MODEL), MIX_WIDTH),
        "w_cq": w(ks[4], (DEPTH, D_MODEL, CX_WIDTH), D_MODEL),
        "w_ckv": w(ks[5], (DEPTH, D_MODEL, 2 * CX_WIDTH), D_MODEL),
        "w_co": w(ks[6], (DEPTH, CX_WIDTH, D_MODEL), CX_WIDTH),
        "w_ffn_in": w(ks[7], (DEPTH, D_MODEL, 2 * FFN_HIDDEN), D_MODEL),
        "w_ffn_out": w(ks[8], (DEPTH, FFN_HIDDEN, D_MODEL), FFN_HIDDEN),
        "mix_pre_g": gain(ks[9], (DEPTH, D_MODEL)),
        "mix_post_g": gain(ks[10], (DEPTH, D_MODEL)),
        "cross_pre_g": gain(ks[11], (DEPTH, D_MODEL)),
        "cross_post_g": gain(ks[12], (DEPTH, D_MODEL)),
        "mem_norm_g": gain(ks[13], (DEPTH, D_MODEL)),
        "ffn_pre_g": gain(ks[14], (DEPTH, D_MODEL)),
        "ffn_post_g": gain(ks[15], (DEPTH, D_MODEL)),
        "da_subln_g": gain(ks[16], (DEPTH, DA_V_DIM)),
        "hg_onorm_g": gain(ks[17], (DEPTH, HG_DV)),
        "lambda_q1": 0.1 * jax.random.normal(ks[18], (DEPTH, DA_QK_DIM), jnp.float32),
        "lambda_k1": 0.1 * jax.random.normal(ks[19], (DEPTH, DA_QK_DIM), jnp.float32),
        "lambda_q2": 0.1 * jax.random.normal(ks[20], (DEPTH, DA_QK_DIM), jnp.float32),
        "lambda_k2": 0.1 * jax.random.normal(ks[21], (DEPTH, DA_QK_DIM), jnp.float32),
        "hg_lb_logits": 0.5 * jax.random.normal(ks[22], (DEPTH, HG_KEY_COLS), jnp.float32),
        "rel_bias": 0.1 * jax.random.normal(ks[23], (REL_BUCKETS, DA_HEADS), jnp.float32),
    }


def reference(x, mem, w_in, w_out, w_cq, w_ckv, w_co, w_ffn_in, w_ffn_out,
              mix_pre_g, mix_post_g, cross_pre_g, cross_post_g, mem_norm_g,
              ffn_pre_g, ffn_post_g, da_subln_g, hg_onorm_g,
              lambda_q1, lambda_k1, lambda_q2, lambda_k2, hg_lb_logits, rel_bias):
    B, S = x.shape[0], x.shape[1]
    M = mem.shape[1]
    lb_p = jax.nn.softmax(hg_lb_logits.astype(jnp.float32), axis=0)
    lower_bounds = jnp.cumsum(lb_p, axis=0) - lb_p[0:1]

    for l in range(DEPTH):
        h = rms_norm(x, mix_pre_g[l])
        proj = h @ w_in[l]
        q_da, k_da, v_da, f_hg, q_hg, i_hg, g_hg = jnp.split(proj, IN_SPLITS, axis=-1)

        lam_init = 0.8 - 0.6 * math.exp(-0.3 * l)
        lam = (jnp.exp(jnp.sum(lambda_q1[l].astype(jnp.float32) * lambda_k1[l].astype(jnp.float32)))
               - jnp.exp(jnp.sum(lambda_q2[l].astype(jnp.float32) * lambda_k2[l].astype(jnp.float32)))
               + lam_init)
        o_da = differential_attention(
            q_da.reshape(B, S, DA_HEADS, 2, DA_QK_DIM),
            k_da.reshape(B, S, DA_HEADS, 2, DA_QK_DIM),
            v_da.reshape(B, S, DA_HEADS, DA_V_DIM),
            lam, lam_init, da_subln_g[l], rel_bias)
        o_hg = hgrn2(f_hg, q_hg, i_hg, g_hg, lower_bounds[l], hg_onorm_g[l])
        mixed = jnp.concatenate([o_da, o_hg], axis=-1) @ w_out[l]
        x = x + rms_norm(mixed, mix_post_g[l])

        h = rms_norm(x, cross_pre_g[l])
        m = rms_norm(mem, mem_norm_g[l])
        qx = (h @ w_cq[l]).reshape(B, S, CX_HEADS, CX_HEAD_DIM)
        kv = (m @ w_ckv[l]).reshape(B, M, 2, CX_HEADS, CX_HEAD_DIM)
        s = jnp.einsum('bqhd,bkhd->bhqk', qx, kv[:, :, 0]).astype(jnp.float32) * CX_HEAD_DIM ** -0.5
        p = jax.nn.softmax(s, axis=-1)
        o = jnp.einsum('bhqk,bkhd->bqhd', p.astype(x.dtype), kv[:, :, 1]).reshape(B, S, CX_WIDTH)
        x = x + rms_norm(o @ w_co[l], cross_post_g[l])

        h = rms_norm(x, ffn_pre_g[l])
        gate, up = jnp.split(h @ w_ffn_in[l], 2, axis=-1)
        x = x + rms_norm((jax.nn.silu(gate) * up) @ w_ffn_out[l], ffn_post_g[l])

    return x
```

```python
import numpy as np
import concourse.bass as bass
import concourse.mybir as mybir

F32 = mybir.dt.float32
BF16 = mybir.dt.bfloat16
AF = mybir.ActivationFunctionType
ALU = mybir.AluOpType
AX = mybir.AxisListType


class Tl:
    def __init__(self, K, name, ap, dma_sem=None):
        self.K = K
        self.name = name
        self.ap = ap
        self.writers = {}
        self.reads = {}
        self.dsem = dma_sem
        self.dcount = 0
        self.is_dram = False
        self.psum = False

    def __getitem__(self, idx):
        return V(self, self.ap[idx])

    def v(self):
        return V(self, self.ap)


class V:
    def __init__(self, t, ap):
        self.t = t
        self.ap = ap

    def __getitem__(self, idx):
        return V(self.t, self.ap[idx])

    def re(self, pat, **kw):
        return V(self.t, self.ap.rearrange(pat, **kw))

    def bc(self, shape):
        return V(self.t, self.ap.broadcast_to(shape))


class Eng:
    def __init__(self, name, h, sem):
        self.name = name
        self.h = h
        self.sem = sem
        self.count = 0
        self.known = {}
        self.prog = []

    def replay(self, h):
        for it in self.prog:
            if it[0] == 'w':
                h.wait_ge(it[1], it[2])
            else:
                it[1](h).then_inc(it[2], it[3])


class K:
    def __init__(self, nc, block_engines, sems):
        self.nc = nc
        self.E = {}
        self.sems = sems
        self.pool = []
        self.tiles = []
        self.phase_tiles = []
        self.n_wait = 0
        self.n_ops = 0

    def get_dsem(self):
        if self.pool:
            return self.pool.pop()
        return (next(self.sems), 0)

    def end_phase(self):
        evs = [(t.dsem, t.dcount) for t in self.phase_tiles if t.dsem is not None]
        self._wait(self.E['sync'], evs)
        self.run_block()
        for e in self.E.values():
            e.prog = []
        for t in self.phase_tiles:
            if t.dsem is not None:
                self.pool.append((t.dsem, t.dcount))
        self.phase_tiles = []

    def add_engine(self, name, handle):
        self.E[name] = Eng(name, handle, next(self.sems))

    def tile(self, name, ap, dram=False):
        if type(ap).__name__ != 'AP':
            ap = ap[:]
        t = Tl(self, name, ap)
        t.is_dram = dram
        if not dram:
            self.phase_tiles.append(t)
        return t

    def run_block(self):
        nc = self.nc
        with nc.Block() as block:
            for name, deco in (('pe', block.tensor), ('act', block.scalar), ('dve', block.vector),
                               ('pool', block.gpsimd), ('sync', block.sync)):
                if name in self.E:
                    e = self.E[name]
                    deco(lambda h, e=e: e.replay(h))

    def _deps(self, reads, writes):
        evs = []
        for v in reads:
            evs.extend(v.t.writers.values())
        for v in writes:
            evs.extend(v.t.writers.values())
            evs.extend(v.t.reads.values())
        return evs

    def _wait(self, e, evs, skip_self=False):
        need = {}
        for (sem, val) in evs:
            if skip_self and sem is e.sem:
                continue
            if e.known.get(id(sem), 0) >= val:
                continue
            if need.get(id(sem), (None, 0))[1] < val:
                need[id(sem)] = (sem, val)
        for sem, val in need.values():
            e.prog.append(('w', sem, val))
            e.known[id(sem)] = val
            self.n_wait += 1

    def _done(self, ev, reads, writes):
        for v in writes:
            v.t.writers[id(ev[0])] = ev
            v.t.reads = {}
        for v in reads:
            v.t.reads[id(ev[0])] = ev

    def op(self, eng, fn, reads, writes, skip_self=False):
        e = self.E[eng]
        pr = [v for v in reads if v.t.psum]
        if pr:
            reads = [v for v in reads if not v.t.psum]
            writes = list(writes) + pr
        self._wait(e, self._deps(reads, writes), skip_self=skip_self)
        e.count += 1
        e.prog.append(('o', fn, e.sem, 1))
        ev = (e.sem, e.count)
        self._done(ev, reads, writes)
        self.n_ops += 1

    def dma(self, q, out, in_, sem_tile=None, **kw):
        e = self.E[q]
        st = sem_tile if sem_tile is not None else (in_.t if out.t.is_dram else out.t)
        if st.dsem is None:
            st.dsem, st.dcount = self.get_dsem()
        reads, writes = [in_], [out]
        evs = self._deps(reads, writes)
        self._wait(e, evs)
        oap, iap = out.ap, in_.ap
        e.prog.append(('o', (lambda h: h.dma_start(out=oap, in_=iap, **kw)), st.dsem, 16))
        st.dcount += 16
        ev = (st.dsem, st.dcount)
        self._done(ev, reads, writes)
        self.n_ops += 1

    def wait_all(self, eng, tiles):
        e = self.E[eng]
        evs = []
        for t in tiles:
            evs.extend(t.writers.values())
            evs.extend(t.reads.values())
        self._wait(e, evs)

    def mm(self, out, lhsT, rhs, start=True, stop=True, extra_reads=()):
        return self.op('pe', lambda h: h.matmul(out.ap, lhsT.ap, rhs.ap, start=start, stop=stop),
                       [lhsT, rhs, *extra_reads], [out], skip_self=True)

    def tr(self, out, in_, ident):
        return self.op('pe', lambda h: h.transpose(out.ap, in_.ap, ident.ap), [in_, ident], [out], skip_self=True)

    def act(self, out, in_, func, bias=None, scale=None, accum=None, eng='act'):
        kw = {}
        reads = [in_]
        writes = [out]
        if bias is not None:
            if isinstance(bias, V):
                kw['bias'] = bias.ap
                reads.append(bias)
            else:
                kw['bias'] = bias
        if scale is not None:
            if isinstance(scale, V):
                kw['scale'] = scale.ap
                reads.append(scale)
            else:
                kw['scale'] = scale
        if accum is not None:
            kw['accum_out'] = accum.ap
            writes.append(accum)
        return self.op(eng, lambda h: h.activation(out.ap, in_.ap, func, **kw), reads, writes)

    def tt(self, out, a, b, op, eng='dve'):
        return self.op(eng, lambda h: h.tensor_tensor(out.ap, a.ap, b.ap, op), [a, b], [out])

    def ts(self, out, a, s1, s2, op0, op1=None, eng='dve', accum=None):
        reads = [a]
        writes = [out]

        def g(s):
            if isinstance(s, V):
                reads.append(s)
                return s.ap
            return s
        s1a, s2a = g(s1), g(s2)
        kw = {}
        if op1 is not None:
            kw['op1'] = op1
        if accum is not None:
            kw['accum_out'] = accum.ap
            writes.append(accum)
        return self.op(eng, lambda h: h.tensor_scalar(out.ap, a.ap, s1a, s2a, op0, **kw), reads, writes)

    def stt(self, out, a, s, b, op0, op1, eng='dve', accum=None):
        reads = [a, b]
        writes = [out]
        sa = s
        if isinstance(s, V):
            reads.append(s)
            sa = s.ap
        kw = {}
        if accum is not None:
            kw['accum_out'] = accum.ap
            writes.append(accum)
        return self.op(eng, lambda h: h.scalar_tensor_tensor(out.ap, a.ap, sa, b.ap, op0, op1, **kw), reads, writes)

    def cp(self, out, in_, eng='dve'):
        if eng == 'act':
            return self.op(eng, lambda h: h.copy(out.ap, in_.ap), [in_], [out])
        return self.op(eng, lambda h: h.tensor_copy(out.ap, in_.ap), [in_], [out])

    def memset(self, out, val, eng='dve'):
        return self.op(eng, lambda h: h.memset(out.ap, val), [], [out])

    def cc(self, kind, ins_t, outs_t, groups):
        iap, oap = ins_t.ap, outs_t.ap
        self.op('pool', lambda h: h.collective_compute(kind, ALU.bypass, replica_groups=groups, ins=[iap], outs=[oap]),
                [ins_t.v()], [outs_t.v()])
import math
import contextlib
from concourse.bass_utils import run_bass_kernel_spmd

T = 2048
D = 2048
NT = 16
EPS = 1e-6
FH = 5632
NFC = 44
LAM_INIT = [0.8 - 0.6 * math.exp(-0.3 * l) for l in range(2)]
PAIRS = [[0, 1], [2, 3], [4, 5], [6, 7]]
NEG = -30000.0

W_SPECS = [
    ("w_in", 2048, 7168), ("w_out", 2048, 2048), ("w_cq", 2048, 512), ("w_ckv", 2048, 1024),
    ("w_co", 512, 2048), ("w_ffn_in", 2048, 11264), ("w_ffn_out", 5632, 2048)]
WDIM = {n: (kd, nn) for (n, kd, nn) in W_SPECS}
CW = {"w_in": 1024, "w_out": 2048, "w_cq": 512, "w_ckv": 1024, "w_co": 2048, "w_ffn_in": 1408, "w_ffn_out": 512}
G_NAMES = ["mix_pre_g", "mix_post_g", "cross_pre_g", "cross_post_g", "mem_norm_g", "ffn_pre_g", "ffn_post_g"]


class Prog:
    def __init__(self, dbg=(), stop=None, inject=(), noweights=False):
        self.dbg = set(dbg)
        self.stop = stop
        nc = bass.Bass("TRN2", target_bir_lowering=False)
        self.nc = nc
        self.es = contextlib.ExitStack()
        def semgen():
            i = 0
            while True:
                i += 1
                assert i < 90, "out of semaphores"
                yield nc.alloc_semaphore(name=f"s{i}")
        sems = semgen()
        k = K(nc, None, sems)
        self.k = k
        for n, h in (('pe', nc.tensor), ('act', nc.scalar), ('dve', nc.vector), ('pool', nc.gpsimd), ('sync', nc.sync)):
            k.add_engine(n, h)
        self.dr = {}

        def inp(name, shape, dt=F32):
            self.dr[name] = k.tile(name, nc.dram_tensor(name, list(shape), dt, kind="ExternalInput").ap(), dram=True)

        def scr(name, shape, dt):
            kind = "ExternalOutput" if name in self.dbg else ("ExternalInput" if name in inject else "Internal")
            self.dr[name] = k.tile(name, nc.dram_tensor(name, list(shape), dt, kind=kind).ap(), dram=True)

        inp("x", [T, D]); inp("mem", [256, D])
        for (n, kd, nn) in W_SPECS:
            if not noweights:
                inp(n, [2, kd, nn])
        for n in ("mix_post_g", "cross_post_g", "ffn_post_g"):
            inp(n, [2, D])
        for n in ("mix_pre_g", "cross_pre_g", "mem_norm_g", "ffn_pre_g"):
            inp(n, [2, D])
        inp("da_subln_g", [2, 128]); inp("hg_onorm_g", [2, 128])
        for n in ("lambda_q1", "lambda_k1", "lambda_q2", "lambda_k2"):
            inp(n, [2, 64])
        inp("hg_lb_logits", [2, 1024]); inp("rel_bias", [32, 8])
        inp("biasT", [2, 128, 8, 128]); inp("flag", [1, 1]); inp("negmask", [1, 1])
        self.dr["out"] = k.tile("out", nc.dram_tensor("out", [T, D], F32, kind="ExternalOutput").ap(), dram=True)
        for l in range(2):
            for (n, kd, nn) in W_SPECS:
                scr(f"b_{n}{l}", [kd, nn], BF16)
        scr("xs", [T, D], F32); scr("Y", [T, D], F32)
        scr("QT", [8, 128, T], BF16);
        for j in range(4):
            scr(f"cckv{j}_in", [512, 2048], BF16); scr(f"cckv{j}_out", [1024, 2048], BF16)
        scr("FT", [8, 128, T], F32); scr("QH", [8, 128, T], BF16)
        scr("IH", [T, 1024], BF16); scr("GH", [T, 1024], BF16)
        scr("OL", [T, 1024], F32); scr("QC", [8, 128, T], BF16)
        scr("CCS_in", [1024, 128], F32); scr("CCS_out", [2048, 128], F32)
        scr("OM", [T, 1024], BF16)
        for n in ("dx1", "dx2", "dx3"):
            if n in self.dbg:
                scr(n, [T, D], F32)

    def alloc(self, es):
        nc, k = self.nc, self.k
        cnt = [0]

        def sb(shape, dt, name=None):
            cnt[0] += 1
            nm = name or f"t{cnt[0]}"
            return k.tile(nm, es.enter_context(nc.sbuf_tensor(f"{nm}_{self.k.n_ops}", list(shape), dt)))

        def ps(shape, dt, name=None):
            cnt[0] += 1
            nm = name or f"p{cnt[0]}"
            t = k.tile(nm, es.enter_context(nc.psum_tensor(f"{nm}_{self.k.n_ops}", list(shape), dt)))
            t.psum = True
            return t
        return sb, ps

    def make_ident(self, sb, dt=BF16):
        k = self.k
        idf = sb([128, 128], F32, "idf")
        k.memset(idf.v(), 1.0, eng='pool')
        k.op('pool', lambda h: h.affine_select(idf.ap, idf.ap, [[1, 128]], ALU.is_equal, 0.0, base=0, channel_multiplier=-1),
             [idf.v()], [idf.v()])
        if dt == F32:
            return idf
        idb = sb([128, 128], BF16, "idb")
        k.cp(idb.v(), idf.v())
        return idb

    def rstd(self, out, ss, n, epst):
        k = self.k
        k.act(out, ss, AF.Sqrt, bias=epst, scale=1.0 / n)
        k.op('dve', lambda h: h.reciprocal(out.ap, out.ap), [out], [out])

    def wload(self, dst, n, l, c0, w, q='sync'):
        self.k.dma(q, dst, self.dr[f"b_{n}{l}"].v().re("(k p) c -> p k c", p=128)[:, :, c0:c0 + w])

    def phaseW(self, l, engs=('pool', 'dve', 'act')):
        k, dr = self.k, self.dr
        with contextlib.ExitStack() as es:
            sb, ps = self.alloc(es)
            self.emitW(l, sb, engs)
            k.end_phase()

    def emitW(self, l, sb, engs):
        k, dr = self.k, self.dr
        NB = 4
        st = [sb([128, 2048], F32) for _ in range(NB)]
        wb = [sb([128, 2048], BF16) for _ in range(NB)]
        gcol = {}
        for gname in ("mix_pre_g", "cross_pre_g", "mem_norm_g", "ffn_pre_g"):
            g = sb([128, 16], F32)
            k.dma('sync', g.v(), dr[gname][l, :].re("(kc p) -> p kc", p=128), allow_slow_non_contiguous=True)
            gcol[gname] = g
        g2 = sb([128, 2], F32)
        k.dma('sync', g2[:, 0:1], dr["da_subln_g"][l, :].re("(p o) -> p o", o=1))
        k.dma('sync', g2[:, 1:2], dr["hg_onorm_g"][l, :].re("(p o) -> p o", o=1))
        k.ts(g2[:, 0:1], g2[:, 0:1], 1.0 - LAM_INIT[l], None, ALU.mult)
        scale_of = {"w_in": "mix_pre_g", "w_cq": "cross_pre_g", "w_ckv": "mem_norm_g", "w_ffn_in": "ffn_pre_g"}
        items = []
        for (n, kd, nn) in W_SPECS:
            for kc in range(kd // 128):
                for c0 in range(0, nn, 2048):
                    w = min(2048, nn - c0)
                    if n in scale_of:
                        sc = gcol[scale_of[n]][:, kc:kc + 1]
                    elif n == "w_out":
                        sc = g2[:, 0:1] if kc < 8 else g2[:, 1:2]
                    else:
                        sc = None
                    items.append((n, kc, c0, w, sc))
        LAG = 2
        for i in range(len(items) + LAG):
            if i < len(items):
                n, kc, c0, w, sc = items[i]
                k.dma('sync', st[i % NB][:, 0:w], dr[n][l, kc * 128:(kc + 1) * 128, c0:c0 + w])
            j = i - LAG
            if j >= 0:
                n, kc, c0, w, sc = items[j]
                e = engs[j % len(engs)]
                src, dst = st[j % NB][:, 0:w], wb[j % NB][:, 0:w]
                if sc is None:
                    k.cp(dst, src, eng=e)
                elif e == 'act':
                    k.act(dst, src, AF.Copy, scale=sc)
                else:
                    k.ts(dst, src, sc, None, ALU.mult, eng=e)
                k.dma('sync', dr[f"b_{n}{l}"][kc * 128:(kc + 1) * 128, c0:c0 + w], dst)

    def norm_pass(self, es_outer, x_src, y, gpost, x_dst, want_hT, dbg_dst=None):
        k, dr = self.k, self.dr
        hT = None
        if want_hT:
            sbo, _ = self.alloc(es_outer)
            hT = sbo([128, 16, T], BF16, "hT")
        with contextlib.ExitStack() as es:
            sb, ps = self.alloc(es)
            xt = [sb([128, D], F32) for _ in range(2)]
            yt = [sb([128, D], F32) for _ in range(2)] if y is not None else None
            junk = sb([128, D], BF16)
            hb = [sb([128, D], BF16) for _ in range(2)]
            st_ = [sb([128, 4], F32) for _ in range(2)]
            epst = sb([128, 1], F32)
            k.memset(epst.v(), EPS)
            if y is not None:
                gp = sb([128, D], F32)
                k.dma('sync', gp.v(), gpost.bc([128, D]))
            if want_hT:
                idb = self.make_ident(sb)
                ptr = [ps([128, 4, 128], BF16) for _ in range(2)]
            ntr = 0
            for tt in range(NT):
                rows = slice(tt * 128, (tt + 1) * 128)
                x_, s_ = xt[tt % 2], st_[tt % 2]
                k.dma('sync', x_.v(), x_src[rows, :])
                if y is not None:
                    y_ = yt[tt % 2]
                    k.dma('sync', y_.v(), y[rows, :])
                    k.act(junk.v(), y_.v(), AF.Square, accum=s_[:, 0:1])
                    self.rstd(s_[:, 1:2], s_[:, 0:1], D, epst.v())
                    k.stt(y_.v(), y_.v(), s_[:, 1:2], gp.v(), ALU.mult, ALU.mult)
                    k.tt(x_.v(), x_.v(), y_.v(), ALU.add, eng='pool')
                    k.dma('sync', x_dst[rows, :], x_.v())
                    if dbg_dst is not None:
                        k.dma('sync', dbg_dst[rows, :], x_.v())
                if want_hT:
                    h_ = hb[tt % 2]
                    k.act(junk.v(), x_.v(), AF.Square, accum=s_[:, 2:3])
                    self.rstd(s_[:, 3:4], s_[:, 2:3], D, epst.v())
                    k.ts(h_.v(), x_.v(), s_[:, 3:4], None, ALU.mult)
                    for g4 in range(4):
                        p_ = ptr[ntr % 2]
                        for j in range(4):
                            kc = g4 * 4 + j
                            k.tr(p_[:, j, :], h_[:, kc * 128:(kc + 1) * 128], idb.v())
                        k.cp(hT[:, g4 * 4:(g4 + 1) * 4, rows], p_.v(), eng=('act' if ntr % 2 else 'dve'))
                        ntr += 1
            k.end_phase()
        return hT

    def phaseA(self, l, hT):
        k, dr = self.k, self.dr
        Vview = [dr[f"cckv{2 + j}_in"].v().re("r (two c) -> (r two) c", two=2) for j in range(2)]
        with contextlib.ExitStack() as es:
            sb, ps = self.alloc(es)
            wt = [sb([128, 16, 512], BF16) for _ in range(2)]
            pa = [ps([128, 512], F32) for _ in range(4)]
            stg32 = [sb([128, T], F32) for _ in range(2)]
            stg16 = [sb([128, T], BF16) for _ in range(2)]
            stk = [sb([128, 512], BF16) for _ in range(3)]
            npa = 0
            nst = 0
            self.wload(wt[0].v(), "w_in", l, 0, 512)
            for g in range(14):
                if g + 1 < 14:
                    self.wload(wt[(g + 1) % 2].v(), "w_in", l, (g + 1) * 512, 512)
                w_ = wt[g % 2]
                kind = g // 2
                if kind in (0, 1, 3, 4):
                    for cc in range(4):
                        h = (g % 2) * 4 + cc
                        is32 = (kind == 3)
                        sg_ = (stg32 if is32 else stg16)[nst % 2]
                        nst += 1
                        for tb in range(4):
                            p_ = pa[npa % 4]
                            for kc in range(16):
                                k.mm(p_.v(), w_[:, kc, cc * 128:(cc + 1) * 128], hT[:, kc, tb * 512:(tb + 1) * 512],
                                     start=(kc == 0), stop=(kc == 15))
                            k.cp(sg_[:, tb * 512:(tb + 1) * 512], p_.v(), eng=('act' if npa % 2 else 'dve'))
                            npa += 1
                        if kind == 0:
                            dst = dr["QT"][h]
                        elif kind == 1:
                            dst = dr[f"cckv{h // 4}_in"][(h % 4) * 128:(h % 4 + 1) * 128, :]
                        elif kind == 3:
                            dst = dr["FT"][h]
                        else:
                            dst = dr["QH"][h]
                        k.dma('sync', dst, sg_.v())
                else:
                    for tt in range(NT):
                        p_ = pa[npa % 4]
                        for kc in range(16):
                            k.mm(p_.v(), hT[:, kc, tt * 128:(tt + 1) * 128], w_[:, kc, :], start=(kc == 0), stop=(kc == 15))
                        s_ = stk[npa % 3]
                        k.cp(s_.v(), p_.v(), eng=('act' if npa % 2 else 'dve'))
                        npa += 1
                        cols = slice((g % 2) * 512, (g % 2) * 512 + 512)
                        rows = slice(tt * 128, (tt + 1) * 128)
                        if kind == 2:
                            dst = Vview[tt // 8][(tt % 8) * 128:(tt % 8 + 1) * 128, cols]
                        elif kind == 5:
                            dst = dr["IH"][rows, cols]
                        else:
                            dst = dr["GH"][rows, cols]
                        k.dma('sync', dst, s_.v())
            k.end_phase()

    def phaseB(self, l):
        k, dr = self.k, self.dr
        with contextlib.ExitStack() as es:
            sb, ps = self.alloc(es)
            idb = self.make_ident(sb)
            mask = sb([64, 64], F32)
            k.memset(mask.v(), 1.0, eng='pool')
            k.op('pool', lambda h: h.affine_select(mask.ap, mask.ap, [[1, 64]], ALU.is_ge, 0.0, base=0, channel_multiplier=-1),
                 [mask.v()], [mask.v()])
            lbc = sb([128, 8], F32)
            oml = sb([128, 8], F32)
            if l == 0:
                k.memset(lbc.v(), 0.0)
            else:
                lg = sb([128, 2, 8], F32)
                k.dma('sync', lg.v(), dr["hg_lb_logits"].v().re("l (h p) -> p l h", p=128), allow_slow_non_contiguous=True)
                k.tt(lbc.v(), lg[:, l, :], lg[:, 0, :], ALU.subtract)
                k.act(lbc.v(), lbc.v(), AF.Sigmoid)
            k.ts(oml.v(), lbc.v(), -1.0, 1.0, ALU.mult, ALU.add)
            NS = 2
            S_ = []
            for s_i in range(NS):
                d = dict(
                    ft=sb([128, T], F32), t1=sb([128, T], F32), t2=sb([128, T], F32), t3=sb([128, T], F32),
                    qh=sb([128, T], BF16), qt=sb([128, T], BF16), kt=sb([128, T], BF16), qc=sb([128, T], BF16),
                    vv=sb([64, 32, 128], BF16), ost=sb([64, 32, 128], F32),
                    base=sb([128, 32], F32), sm=sb([128, 3, 32], F32), AGH=sb([128, 3, 32], F32),
                    S=sb([128, 128], F32), Sb=sb([128, 128], BF16), ktok=sb([64, 128], BF16), attn=sb([64, 64], BF16),
                    tkv=sb([128, 128], F32),
                    pAT=ps([64, 512], F32), pKT=ps([64, 1024], BF16), pO=ps([64, 512], F32), pKV=ps([128, 512], F32))
                S_.append(d)
            for h0 in range(0, 8, NS):
                for s_i in range(NS):
                    h = h0 + s_i
                    d = S_[s_i]
                    k.dma('sync', d['ft'].v(), dr["FT"][h])
                    k.dma('sync', d['qh'].v(), dr["QH"][h])
                    k.dma('sync', d['vv'].v(), dr["IH"][:, h * 128:(h + 1) * 128].re("(n s) v -> s n v", s=64))
                    t1, t2, t3, ft = d['t1'], d['t2'], d['t3'], d['ft']
                    k.act(t1.v(), ft.v(), AF.Sigmoid)
                    k.ts(t1.v(), t1.v(), oml[:, h:h + 1], lbc[:, h:h + 1], ALU.mult, ALU.add)
                    k.act(t2.v(), t1.v(), AF.Ln)
                    k.ts(t1.v(), t1.v(), -1.0, 1.0, ALU.mult, ALU.add, eng='pool')
                    k.op('dve', lambda hh, o=t3.ap, a=t2.ap: hh.tensor_tensor_scan(o, a, a, 0.0, ALU.add, ALU.bypass),
                         [t2.v()], [t3.v()])
                    bg3 = t3.v().re("p (n c) -> p n c", c=64)
                    mids, ends = bg3[:, :, 31], bg3[:, :, 63]
                    k.memset(d['base'][:, 0:1], 0.0)
                    k.cp(d['base'][:, 1:32], bg3[:, 0:31, 63])
                    k.tt(d['sm'][:, 0, :], mids, d['base'].v(), ALU.subtract)
                    k.tt(d['sm'][:, 1, :], ends, d['base'].v(), ALU.subtract)
                    k.tt(d['sm'][:, 2, :], ends, mids, ALU.subtract)
                    k.act(d['AGH'].v(), d['sm'].v(), AF.Exp)
                    k.act(t2.v(), t3.v(), AF.Exp)
                    k.tt(d['qc'].v(), d['qh'].v(), t2.v(), ALU.mult, eng='pool')
                    k.dma('sync', dr["QC"][h], d['qc'].v())
                    k.tt(t2.v().re("p (n c) -> p n c", c=64), bg3, bg3[:, :, 31:32].bc([128, 32, 64]), ALU.subtract)
                    k.act(ft.v(), t2.v(), AF.Exp)
                    k.act(t3.v(), t2.v(), AF.Exp, scale=-1.0)
                    k.tt(d['qt'].v(), d['qh'].v(), ft.v(), ALU.mult)
                    k.tt(d['kt'].v(), t1.v(), t3.v(), ALU.mult, eng='pool')
                    k.memset(d['S'].v(), 0.0)
                    k.memset(d['Sb'].v(), 0.0)
                for n in range(32):
                    cs = slice(n * 64, (n + 1) * 64)
                    for s_i in range(NS):
                        d = S_[s_i]
                        A, G, H = d['AGH'][:, 0, :], d['AGH'][:, 1, :], d['AGH'][:, 2, :]
                        k.mm(d['pAT'][:, 0:64], d['kt'][:, cs], d['qt'][:, cs])
                        k.tt(d['attn'].v(), d['pAT'][:, 0:64], mask.v(), ALU.mult)
                        k.tr(d['pKT'][:, 0:128], d['kt'][:, cs], idb.v())
                        k.cp(d['ktok'].v(), d['pKT'][:, 0:128], eng='act')
                        k.mm(d['pO'][:, 0:128], d['attn'].v(), d['vv'][:, n, :], start=True, stop=False)
                        k.mm(d['pO'][:, 0:128], d['qt'][:, cs], d['Sb'].v(), start=False, stop=True)
                        k.cp(d['ost'][:, n, :], d['pO'][:, 0:128], eng='act')
                        k.mm(d['pKV'][:, 0:128], d['ktok'].v(), d['vv'][:, n, :])
                        k.ts(d['tkv'].v(), d['pKV'][:, 0:128], H[:, n:n + 1], None, ALU.mult)
                        k.stt(d['S'].v(), d['S'].v(), G[:, n:n + 1], d['tkv'].v(), ALU.mult, ALU.add)
                        if n < 31:
                            k.act(d['Sb'].v(), d['S'].v(), AF.Copy, scale=A[:, n + 1:n + 2])
                for s_i in range(NS):
                    h = h0 + s_i
                    d = S_[s_i]
                    k.dma('sync', dr["OL"][:, h * 128:(h + 1) * 128].re("(n t) v -> t n v", t=64), d['ost'].v())
                    k.dma('sync', dr["CCS_in"][h * 128:(h + 1) * 128, :], d['S'].v())
            k.end_phase()
        k.cc("AllGather", dr["CCS_in"], dr["CCS_out"], PAIRS)

    def phaseC(self, l):
        k, dr = self.k, self.dr
        Vown = [dr[f"cckv{2 + j}_in"].v().re("r (two c) -> (r two) c", two=2) for j in range(2)]
        Vprev = [dr[f"cckv{2 + j}_out"][0:512, :].re("r (two c) -> (r two) c", two=2) for j in range(2)]
        with contextlib.ExitStack() as es:
            sb, ps = self.alloc(es)
            epst = sb([128, 1], F32)
            k.memset(epst.v(), EPS)
            lq = sb([128, 4, 64], F32)
            for i_, nme in enumerate(("lambda_q1", "lambda_k1", "lambda_q2", "lambda_k2")):
                k.dma('sync', lq[:, i_, :], dr[nme][l:l + 1, :].bc([128, 64]))
            lj = sb([128, 64], F32)
            ls = sb([128, 4], F32)
            k.stt(lj.v(), lq[:, 0, :], 1.0, lq[:, 1, :], ALU.mult, ALU.mult, accum=ls[:, 0:1])
            k.stt(lj.v(), lq[:, 2, :], 1.0, lq[:, 3, :], ALU.mult, ALU.mult, accum=ls[:, 1:2])
            k.act(ls[:, 2:4], ls[:, 0:2], AF.Exp)
            neglam = sb([128, 1], F32)
            k.tt(neglam.v(), ls[:, 3:4], ls[:, 2:3], ALU.subtract)
            k.ts(neglam.v(), neglam.v(), -LAM_INIT[l], None, ALU.add)
            nm_ = sb([128, 1], F32)
            k.dma('sync', nm_.v(), dr["negmask"].v().bc([128, 1]))
            cfar = sb([128, 2, 8], F32)
            k.dma('sync', cfar[:, 1, :], dr["rel_bias"][31:32, :].bc([128, 8]))
            k.ts(cfar[:, 0, :], cfar[:, 1, :], nm_.v(), None, ALU.add)
            b01 = sb([128, 2, 8, 128], F32)
            k.dma('sync', b01[:, 0], dr["biasT"][0])
            k.dma('sync', b01[:, 1], dr["biasT"][1])
            b1m = sb([128, 8, 128], F32)
            k.ts(b1m.v(), b01[:, 1], nm_.v(), None, ALU.add)
            NS = 2
            qT = [sb([128, T], BF16) for _ in range(NS)]
            kT = [sb([128, 2 * T], BF16) for _ in range(NS)]
            vt = [sb([128, 32, 129], BF16) for _ in range(NS)]
            ostg = [sb([128, 16, 128], BF16) for _ in range(NS)]
            for s_i in range(NS):
                k.memset(vt[s_i][:, :, 128:129], 1.0)
            sT = [ps([128, 2, 512], F32) for _ in range(2)]
            Oacc = ps([128, 8, 256], F32)
            pT = [sb([128, 2, 512], BF16) for _ in range(2)]
            tmpb = [sb([128, 2, 128], F32) for _ in range(2)]
            o_ = [sb([128, 128], F32) for _ in range(2)]
            t1_ = [sb([128, 128], F32) for _ in range(2)]
            junk = sb([128, 128], BF16)
            sm = [sb([128, 6], F32) for _ in range(2)]

            def load(h):
                s_i = h % NS
                k.dma('sync', qT[s_i].v(), dr["QT"][h])
                k.dma('sync', kT[s_i][:, 0:T], dr[f"cckv{h // 4}_out"][(h % 4) * 128:(h % 4 + 1) * 128, :])
                k.dma('sync', kT[s_i][:, T:2 * T], dr[f"cckv{h // 4}_in"][(h % 4) * 128:(h % 4 + 1) * 128, :])
                for j in range(2):
                    k.dma('sync', vt[s_i][:, j * 8:(j + 1) * 8, 0:128], Vprev[j][:, h * 128:(h + 1) * 128].re("(n p) c -> p n c", p=128))
                    k.dma('sync', vt[s_i][:, 16 + j * 8:16 + (j + 1) * 8, 0:128], Vown[j][:, h * 128:(h + 1) * 128].re("(n p) c -> p n c", p=128))
            load(0)
            it = 0
            ne = 0
            for h in range(8):
                if h + 1 < 8:
                    load(h + 1)
                s_i = h % NS
                for jb in range(4):
                    nkb = 16 + 4 * jb + 4
                    for kb in range(nkb):
                        imin = max(0, kb - 16 - 4 * jb)
                        ifar = min(4, max(imin, kb + 2 - 16 - 4 * jb))
                        sT_, pT_ = sT[it % 2], pT[it % 2]
                        for m in range(2):
                            k.mm(sT_[:, m, imin * 128:512], kT[s_i][m * 64:(m + 1) * 64, kb * 128:(kb + 1) * 128],
                                 qT[s_i][m * 64:(m + 1) * 64, jb * 512 + imin * 128:(jb + 1) * 512])
                        if ifar < 4:
                            k.act(pT_[:, :, ifar * 128:512], sT_[:, :, ifar * 128:512], AF.Exp,
                                  bias=cfar[:, (0 if kb < 16 else 1), h:h + 1], scale=0.125)
                        for i in range(imin, ifar):
                            delta = 16 + 4 * jb + i - kb
                            tb_ = tmpb[ne % 2]
                            ne += 1
                            if delta == 0:
                                bt = b01[:, 0, h, :]
                            elif kb == 15:
                                bt = b1m[:, h, :]
                            else:
                                bt = b01[:, 1, h, :]
                            k.stt(tb_.v(), sT_[:, :, i * 128:(i + 1) * 128], 0.125, bt_b(bt),
                                  ALU.mult, ALU.add)
                            k.act(pT_[:, :, i * 128:(i + 1) * 128], tb_.v(), AF.Exp)
                        for i in range(imin, 4):
                            last = 16 + 4 * jb + i
                            for m in range(2):
                                k.mm(Oacc[:, i * 2 + m, 0:129], pT_[:, m, i * 128:(i + 1) * 128], vt[s_i][:, kb, :],
                                     start=(kb == 0 and m == 0), stop=(kb == last))
                        it += 1
                    for i in range(4):
                        s_ = sm[i % 2]
                        oo, tt1 = o_[i % 2], t1_[i % 2]
                        k.op('dve', lambda hh, o=s_.ap[:, 0:2], a=Oacc.ap[:, i * 2:i * 2 + 2, 128]: hh.reciprocal(o, a),
                             [Oacc.v()], [s_.v()])
                        k.ts(s_[:, 1:2], s_[:, 1:2], neglam.v(), None, ALU.mult)
                        k.act(tt1.v(), Oacc[:, i * 2 + 1, 0:128], AF.Copy, scale=s_[:, 1:2])
                        k.stt(oo.v(), Oacc[:, i * 2, 0:128], s_[:, 0:1], tt1.v(), ALU.mult, ALU.add)
                        k.act(junk.v(), oo.v(), AF.Square, accum=s_[:, 2:3])
                        self.rstd(s_[:, 3:4], s_[:, 2:3], 128, epst.v())
                        k.ts(ostg[s_i][:, jb * 4 + i, :], oo.v(), s_[:, 3:4], None, ALU.mult)
                k.dma('sync', dr["OM"][:, h * 128:(h + 1) * 128].re("(n p) c -> p n c", p=128), ostg[s_i].v())
            k.end_phase()

    def phaseD1(self, l):
        k, dr = self.k, self.dr
        with contextlib.ExitStack() as es:
            sb, ps = self.alloc(es)
            idb = self.make_ident(sb)
            epst = sb([128, 1], F32)
            k.memset(epst.v(), EPS)
            Wb = sb([128, 16, D], BF16)
            for c0 in range(0, D, 512):
                self.wload(Wb[:, :, c0:c0 + 512], "w_out", l, c0, 512)
            fl = sb([128, 1], F32)
            k.dma('sync', fl.v(), dr["flag"].v().bc([128, 1]))
            Sp32 = sb([128, 8, 128], F32)
            k.dma('sync', Sp32.v(), dr["CCS_out"][0:1024, :].re("(h c) v -> c h v", c=128))
            Sp = sb([128, 8, 128], BF16)
            k.ts(Sp.v(), Sp32.v(), fl.v(), None, ALU.mult)
            om = [sb([128, D], BF16) for _ in range(2)]
            ol = [sb([128, 1024], F32) for _ in range(2)]
            gh = [sb([128, 1024], BF16) for _ in range(2)]
            qc = [sb([128, 8, 128], BF16) for _ in range(2)]
            tmp = sb([128, 1024], F32)
            tmp2 = sb([128, 1024], F32)
            st8 = [sb([128, 16], F32) for _ in range(2)]
            omT = [sb([128, 16, 128], BF16) for _ in range(2)]
            ystg = [sb([128, D], F32) for _ in range(2)]
            pc = ps([128, 1024], F32)
            ptr = [ps([128, 4, 128], BF16) for _ in range(2)]
            py = [ps([128, 512], F32) for _ in range(3)]
            ntr = 0
            npy = 0
            for tt in range(NT):
                rows = slice(tt * 128, (tt + 1) * 128)
                b = tt % 2
                k.dma('sync', om[b][:, 0:1024], dr["OM"][rows, :])
                k.dma('sync', ol[b].v(), dr["OL"][rows, :])
                k.dma('sync', gh[b].v(), dr["GH"][rows, :])
                k.dma('sync', qc[b].v(), dr["QC"][:, :, rows].re("h c t -> c h t"))
                for h in range(8):
                    k.mm(pc[:, h * 128:(h + 1) * 128], qc[b][:, h, :], Sp[:, h, :])
                k.tt(ol[b].v(), ol[b].v(), pc.v(), ALU.add)
                k.tt(tmp.v(), ol[b].v(), ol[b].v(), ALU.mult, eng='pool')
                s8 = st8[b]
                k.op('dve', lambda hh, o=s8.ap[:, 0:8], a=tmp.ap.rearrange("p (h v) -> p h v", v=128): hh.tensor_reduce(o, a, AX.X, ALU.add),
                     [tmp.v()], [s8.v()])
                self.rstd(s8[:, 8:16], s8[:, 0:8], 128, epst.v())
                k.act(tmp2.v(), gh[b].v(), AF.Silu)
                k.tt(ol[b].v().re("p (h v) -> p h v", v=128), ol[b].v().re("p (h v) -> p h v", v=128),
                     V(s8, s8.ap[:, 8:16].rearrange("p (h o) -> p h o", o=1).broadcast_to([128, 8, 128])), ALU.mult)
                k.tt(om[b][:, 1024:2048], ol[b].v(), tmp2.v(), ALU.mult)
                oT = omT[b]
                for g4 in range(4):
                    p_ = ptr[ntr % 2]
                    for j in range(4):
                        kc = g4 * 4 + j
                        k.tr(p_[:, j, :], om[b][:, kc * 128:(kc + 1) * 128], idb.v())
                    k.cp(oT[:, g4 * 4:(g4 + 1) * 4, :], p_.v(), eng=('act' if ntr % 2 else 'dve'))
                    ntr += 1
                for cb in range(4):
                    p_ = py[npy % 3]
                    for kc in range(16):
                        k.mm(p_.v(), oT[:, kc, :], Wb[:, kc, cb * 512:(cb + 1) * 512], start=(kc == 0), stop=(kc == 15))
                    k.cp(ystg[b][:, cb * 512:(cb + 1) * 512], p_.v(), eng=('act' if npy % 2 else 'dve'))
                    npy += 1
                k.dma('sync', dr["Y"][rows, :], ystg[b].v())
            k.end_phase()

    def phaseD2(self, l, hT):
        k, dr = self.k, self.dr
        with contextlib.ExitStack() as es:
            sb, ps = self.alloc(es)
            idb = self.make_ident(sb)
            epst = sb([128, 1], F32)
            k.memset(epst.v(), EPS)
            ones = sb([128, 128], BF16)
            k.memset(ones.v(), 1.0)
            Wcq = sb([128, 16, 512], BF16)
            self.wload(Wcq.v(), "w_cq", l, 0, 512)
            Wco = sb([128, 4, D], BF16)
            self.wload(Wco.v(), "w_co", l, 0, D)
            KmT = sb([128, 4, 256], BF16)
            Vm = sb([128, 2, 512], BF16)
            pq = [ps([128, 512], F32) for _ in range(2)]
            pS = [ps([128, 512], F32) for _ in range(2)]
            pO = ps([128, 512], F32)
            pZ = ps([128, 512], F32)
            py = [ps([128, 512], F32) for _ in range(2)]
            with contextlib.ExitStack() as es2:
                sb2, ps2 = self.alloc(es2)
                Wckv = sb2([128, 16, 1024], BF16)
                self.wload(Wckv[:, :, 0:512], "w_ckv", l, 0, 512)
                self.wload(Wckv[:, :, 512:1024], "w_ckv", l, 512, 512)
                mT = sb2([128, 16, 256], BF16)
                mt_ = sb2([128, D], F32)
                mb = sb2([128, D], BF16)
                junk = sb2([128, D], BF16)
                s4 = sb2([128, 4], F32)
                for t_ in range(2):
                    k.dma('sync', mt_.v(), dr["mem"][t_ * 128:(t_ + 1) * 128, :])
                    k.act(junk.v(), mt_.v(), AF.Square, accum=s4[:, 0:1])
                    self.rstd(s4[:, 1:2], s4[:, 0:1], D, epst.v())
                    k.ts(mb.v(), mt_.v(), s4[:, 1:2], None, ALU.mult)
                    for g4 in range(4):
                        p_ = pq[g4 % 2]
                        for j in range(4):
                            kc = g4 * 4 + j
                            k.tr(self._bf(p_)[:, j * 128:(j + 1) * 128], mb[:, kc * 128:(kc + 1) * 128], idb.v())
                        k.cp(mT[:, g4 * 4:(g4 + 1) * 4, t_ * 128:(t_ + 1) * 128],
                             self._bf(p_)[:, 0:512].re("p (j t) -> p j t", t=128))
                for h in range(4):
                    p_ = pq[h % 2]
                    for kc in range(16):
                        k.mm(p_[:, 0:256], Wckv[:, kc, h * 128:(h + 1) * 128], mT[:, kc, :], start=(kc == 0), stop=(kc == 15))
                    k.cp(KmT[:, h, :], p_[:, 0:256])
                for t_ in range(2):
                    p_ = pS[t_]
                    for kc in range(16):
                        k.mm(p_.v(), mT[:, kc, t_ * 128:(t_ + 1) * 128], Wckv[:, kc, 512:1024], start=(kc == 0), stop=(kc == 15))
                    k.cp(Vm[:, t_, :], p_.v())
            qTh = [sb([128, 512], BF16) for _ in range(2)]
            pT = [sb([128, 2, 512], BF16) for _ in range(2)]
            rz = sb([128, 512], F32)
            oT = [sb([128, 4, 512], BF16) for _ in range(2)]
            ystg = [sb([128, D], F32) for _ in range(2)]
            nq = 0
            ny = 0
            for tb in range(4):
                o_ = oT[tb % 2]
                for h in range(4):
                    p_ = pq[nq % 2]
                    q_ = qTh[nq % 2]
                    pt_ = pT[nq % 2]
                    nq += 1
                    for kc in range(16):
                        k.mm(p_.v(), Wcq[:, kc, h * 128:(h + 1) * 128], hT[:, kc, tb * 512:(tb + 1) * 512], start=(kc == 0), stop=(kc == 15))
                    k.cp(q_.v(), p_.v(), eng='act')
                    for mt in range(2):
                        k.mm(pS[mt].v(), KmT[:, h, mt * 128:(mt + 1) * 128], q_.v())
                        k.act(pt_[:, mt, :], pS[mt].v(), AF.Exp, scale=128 ** -0.5)
                    for mt in range(2):
                        k.mm(pO.v(), Vm[:, mt, h * 128:(h + 1) * 128], pt_[:, mt, :], start=(mt == 0), stop=(mt == 1))
                    for mt in range(2):
                        k.mm(pZ.v(), ones.v(), pt_[:, mt, :], start=(mt == 0), stop=(mt == 1))
                    k.op('dve', lambda hh, o=rz.ap, a=pZ.ap: hh.reciprocal(o, a), [pZ.v()], [rz.v()])
                    k.tt(o_[:, h, :], pO.v(), rz.v(), ALU.mult)
                for ts_ in range(4):
                    tt = tb * 4 + ts_
                    ys = ystg[tt % 2]
                    for cb in range(4):
                        p_ = py[ny % 2]
                        for h in range(4):
                            k.mm(p_.v(), o_[:, h, ts_ * 128:(ts_ + 1) * 128], Wco[:, h, cb * 512:(cb + 1) * 512], start=(h == 0), stop=(h == 3))
                        k.cp(ys[:, cb * 512:(cb + 1) * 512], p_.v(), eng=('act' if ny % 2 else 'dve'))
                        ny += 1
                    k.dma('sync', dr["Y"][tt * 128:(tt + 1) * 128, :], ys.v())
            k.end_phase()

    def _bf(self, t):
        if not hasattr(t, '_bfv'):
            t._bfv = V(t, t.ap.bitcast(BF16))
        return t._bfv

    def phaseD3(self, l, hT):
        k, dr = self.k, self.dr
        with contextlib.ExitStack() as es:
            sb, ps = self.alloc(es)
            actT = sb([128, 44, 512], BF16)
            wg = [sb([128, 16, 256], BF16) for _ in range(2)]
            wu = [sb([128, 16, 256], BF16) for _ in range(2)]
            w2 = [sb([128, 44, 256], BF16) for _ in range(2)]
            sg = [sb([128, 512], F32) for _ in range(2)]
            ystg = [sb([128, 256], F32) for _ in range(3)]
            pg = [ps([128, 512], F32) for _ in range(2)]
            pu = [ps([128, 512], F32) for _ in range(2)]
            py = [ps([128, 512], F32) for _ in range(3)]
            groups = []
            for g_ in range(22):
                groups.append((0, 2 * g_, 256 * g_, 256, [(0, 128), (128, 128)]))

            def loadw1(gi):
                r, j0, c0, w, chunks = groups[gi]
                self.wload(wg[gi % 2][:, :, 0:w], "w_ffn_in", l, c0, w)
                self.wload(wu[gi % 2][:, :, 0:w], "w_ffn_in", l, FH + c0, w)
            nf = 0
            ny = 0
            nw2 = 0
            for tb in range(4):
                loadw1(0)
                for gi in range(22):
                    if gi + 1 < 22:
                        loadw1(gi + 1)
                    r, j0, c0, w, chunks = groups[gi]
                    for ci, (o0, sz) in enumerate(chunks):
                        fc = r * 6 + j0 + ci
                        pg_, pu_, sg_ = pg[nf % 2], pu[nf % 2], sg[nf % 2]
                        nf += 1
                        for kc in range(16):
                            k.mm(pg_[0:sz, :], wg[gi % 2][:, kc, o0:o0 + sz], hT[:, kc, tb * 512:(tb + 1) * 512], start=(kc == 0), stop=(kc == 15))
                        for kc in range(16):
                            k.mm(pu_[0:sz, :], wu[gi % 2][:, kc, o0:o0 + sz], hT[:, kc, tb * 512:(tb + 1) * 512], start=(kc == 0), stop=(kc == 15))
                        k.act(sg_[0:sz, :], pg_[0:sz, :], AF.Silu)
                        k.tt(actT[0:sz, fc, :], sg_[0:sz, :], pu_[0:sz, :], ALU.mult)
                for cb2 in range(8):
                    w2_ = w2[nw2 % 2]
                    nw2 += 1
                    self.wload(w2_.v(), "w_ffn_out", l, cb2 * 256, 256)
                    for ts_ in range(4):
                        p_ = py[ny % 3]
                        ys = ystg[ny % 3]
                        for fc in range(44):
                            sz = 128
                            k.mm(p_[:, 0:256], actT[0:sz, fc, ts_ * 128:(ts_ + 1) * 128], w2_[0:sz, fc, :], start=(fc == 0), stop=(fc == 43))
                        k.cp(ys.v(), p_[:, 0:256], eng=('act' if ny % 2 else 'dve'))
                        ny += 1
                        tt = tb * 4 + ts_
                        k.dma('sync', dr["Y"][tt * 128:(tt + 1) * 128, cb2 * 256:(cb2 + 1) * 256], ys.v())
            k.end_phase()

    def build_all(self, nlayers=2, dbgx=None):
        dr = self.dr
        self.phaseW(0)
        if nlayers > 1:
            self.phaseW(1)
        x_cur = dr["x"]
        pend = None
        for l in range(nlayers):
            with contextlib.ExitStack() as eo:
                hT = self.norm_pass(eo, x_cur, *(pend if pend else (None, None)), dr["xs"], True,
                                    dbg_dst=(dr.get("dx3") if l == 1 else None))
                if pend:
                    x_cur = dr["xs"]
                self.phaseA(l, hT)
            for j in range(4):
                self.k.cc("AllGather", dr[f"cckv{j}_in"], dr[f"cckv{j}_out"], PAIRS)
            self.phaseB(l)
            self.phaseC(l)
            self.phaseD1(l)
            with contextlib.ExitStack() as eo:
                hT = self.norm_pass(eo, x_cur, dr["Y"].v(), dr["mix_post_g"][l:l + 1, :], dr["xs"], True,
                                    dbg_dst=(dr.get("dx1") if l == 0 else None))
                x_cur = dr["xs"]
                self.phaseD2(l, hT)
            with contextlib.ExitStack() as eo:
                hT = self.norm_pass(eo, x_cur, dr["Y"].v(), dr["cross_post_g"][l:l + 1, :], dr["xs"], True,
                                    dbg_dst=(dr.get("dx2") if l == 0 else None))
                self.phaseD3(l, hT)
            pend = (dr["Y"].v(), dr["ffn_post_g"][l:l + 1, :])
        with contextlib.ExitStack() as eo:
            self.norm_pass(eo, x_cur, pend[0], pend[1], dr["out"], False)


def bt_b(bt):
    return V(bt.t, bt.ap.rearrange("p (o c) -> p o c", o=1).broadcast_to([128, 2, 128]))


def t5_bucket_np(dist):
    n = np.maximum(dist, 0)
    nf = np.maximum(n, 1).astype(np.float32)
    large = 16 + (np.log(nf / np.float32(16)) / np.float32(math.log(128 / 16)) * np.float32(16)).astype(np.int32)
    large = np.minimum(large, 31)
    return np.where(n < 16, n, large)


def make_in_maps(inputs):
    f = {k_: np.ascontiguousarray(np.asarray(v)) for k_, v in inputs.items()}
    kk = np.arange(128)[:, None]
    qq = np.arange(128)[None, :]
    biasT = np.zeros((2, 128, 8, 128), np.float32)
    for dl in range(2):
        dist = qq - kk + 128 * dl
        b = f["rel_bias"][t5_bucket_np(dist)]
        b = np.where((dist >= 0)[:, :, None], b, np.float32(NEG))
        biasT[dl] = np.transpose(b, (0, 2, 1))
    maps = []
    for c in range(8):
        b, half = c // 2, c % 2
        m = {"x": np.ascontiguousarray(f["x"][b, half * T:(half + 1) * T]), "mem": f["mem"][b]}
        for (n, kd, _) in W_SPECS:
            m[n] = f[n]
        for n in ["mix_pre_g", "cross_pre_g", "mem_norm_g", "ffn_pre_g", "da_subln_g", "hg_onorm_g", "mix_post_g", "cross_post_g", "ffn_post_g", "lambda_q1", "lambda_k1", "lambda_q2", "lambda_k2",
                  "hg_lb_logits", "rel_bias"]:
            m[n] = f[n]
        m["biasT"] = biasT
        m["flag"] = np.full((1, 1), float(half), np.float32)
        m["negmask"] = np.full((1, 1), 0.0 if half else NEG, np.float32)
        maps.append(m)
    return maps


_CACHE = {}


def kernel(**inputs):
    if "nc" not in _CACHE:
        P = Prog()
        P.build_all()
        _CACHE["nc"] = P.nc
    maps = make_in_maps(inputs)
    res = run_bass_kernel_spmd(_CACHE["nc"], maps, core_ids=list(range(8)))
    out = np.zeros((4, 4096, D), np.float32)
    for c in range(8):
        out[c // 2, (c % 2) * T:(c % 2 + 1) * T] = np.asarray(res.results[c]["out"])
    return out
```
